# Optimizing a Trainium2 kernel written in Bass

```python
import math
import jax, jax.numpy as jnp
from jax import lax
import numpy as np

D_MODEL = 2048
BATCH = 1
SEQ = 16384
DEPTH = 4

N_A_LAYERS = DEPTH // 2
N_B_LAYERS = DEPTH - N_A_LAYERS
CHUNK = 128
A_WIDTH = D_MODEL
A_GROUPS = 8
A_GROUP_DIM = A_WIDTH // A_GROUPS
HEAD_DIM = 64
N_Q_HEADS = D_MODEL // HEAD_DIM
N_KV_HEADS = 4
Q_PER_KV = N_Q_HEADS // N_KV_HEADS
WINDOW = 128
ROPE_THETA = 10000.0
D_FF = 4 * D_MODEL
LN_EPS = 1e-5
DEEPNORM_ALPHA = (2.0 * DEPTH) ** 0.25
DEEPNORM_BETA = (8.0 * DEPTH) ** -0.25

kernel_name = "yoco_sgu_swa_sink_deepnorm_trunk"


def layer_norm(x, g, b):
    xf = x.astype(jnp.float32)
    mu = jnp.mean(xf, axis=-1, keepdims=True)
    var = jnp.mean(jnp.square(xf - mu), axis=-1, keepdims=True)
    y = (xf - mu) * lax.rsqrt(var + LN_EPS) * g.astype(jnp.float32) + b.astype(jnp.float32)
    return y.astype(x.dtype)


def rope(t, positions):
    hd = t.shape[-1]
    inv_freq = ROPE_THETA ** (-jnp.arange(0, hd, 2, dtype=jnp.float32) / hd)
    ang = positions.astype(jnp.float32)[:, None] * inv_freq[None, :]
    cos = jnp.cos(ang)[None, :, None, :]
    sin = jnp.sin(ang)[None, :, None, :]
    tf = t.astype(jnp.float32)
    t1, t2 = tf[..., : hd // 2], tf[..., hd // 2:]
    out = jnp.concatenate([t1 * cos - t2 * sin, t2 * cos + t1 * sin], axis=-1)
    return out.astype(t.dtype)


def chunked_sgu(x, w_in, b_in, ln_v_g, ln_v_b, w_s, b_s, w_out):
    B, S, _ = x.shape
    nc = S // CHUNK
    z = jax.nn.gelu(x @ w_in + b_in, approximate=False)
    u, v = jnp.split(z, 2, axis=-1)
    v = layer_norm(v, ln_v_g, ln_v_b)
    v = v.reshape(B, nc, CHUNK, A_GROUPS, A_GROUP_DIM)
    causal = jnp.tril(jnp.ones((CHUNK, CHUNK), dtype=w_s.dtype))
    ws = w_s * causal[None]
    s = jnp.einsum('gij,bnjgd->bnigd', ws, v) + b_s.T[None, None, :, :, None]
    y = u * s.reshape(B, S, A_WIDTH)
    return y @ w_out


def shared_kv_bands(x_kv, w_kv, b_kv, positions):
    B, S, _ = x_kv.shape
    nb = S // WINDOW
    kv = x_kv @ w_kv + b_kv
    k, v = jnp.split(kv, 2, axis=-1)
    k = rope(k.reshape(B, S, N_KV_HEADS, HEAD_DIM), positions)
    v = v.reshape(B, S, N_KV_HEADS, HEAD_DIM)

    def band(t):
        blk = t.reshape(B, nb, WINDOW, N_KV_HEADS, HEAD_DIM)
        prev = jnp.pad(blk[:, :-1], ((0, 0), (1, 0), (0, 0), (0, 0), (0, 0)))
        return jnp.concatenate([prev, blk], axis=2)

    return band(k), band(v)


def sliding_sink_attention(x, k_band, v_band, w_q, b_q, sinks, w_o, positions):
    B, S, _ = x.shape
    nb = S // WINDOW
    q = rope((x @ w_q + b_q).reshape(B, S, N_Q_HEADS, HEAD_DIM), positions)
    q = q.reshape(B, nb, WINDOW, N_KV_HEADS, Q_PER_KV, HEAD_DIM)
    scores = jnp.einsum('bnqhgd,bnkhd->bnhgqk', q, k_band).astype(jnp.float32)
    scores = scores * (HEAD_DIM ** -0.5)
    i = jnp.arange(WINDOW)[:, None]
    j = jnp.arange(2 * WINDOW)[None, :]
    in_band = (j > i) & (j <= i + WINDOW)
    blk = jnp.arange(nb)[:, None, None]
    valid = in_band[None] & ((blk > 0) | (j[None] >= WINDOW))
    scores = jnp.where(valid[None, :, None, None], scores, jnp.finfo(jnp.float32).min)
    sink = jnp.broadcast_to(
        sinks.astype(jnp.float32).reshape(1, 1, N_KV_HEADS, Q_PER_KV, 1, 1),
        scores.shape[:-1] + (1,))
    p = jax.nn.softmax(jnp.concatenate([scores, sink], axis=-1), axis=-1)[..., :-1]
    o = jnp.einsum('bnhgqk,bnkhd->bnqhgd', p.astype(v_band.dtype), v_band)
    return o.reshape(B, S, N_Q_HEADS * HEAD_DIM) @ w_o


def sq_relu_mlp(x, w_up, w_down):
    return jnp.square(jax.nn.relu(x @ w_up)) @ w_down


def setup_inputs(seed: int = 0) -> dict:
    key = jax.random.key(seed)
    ks = jax.random.split(key, 20)
    f32 = jnp.float32

    def nrm(k, shape, scale):
        return jax.random.normal(k, shape, f32) * scale

    x = jax.random.normal(ks[0], (BATCH, SEQ, D_MODEL), f32)
    a_w_in = nrm(ks[1], (N_A_LAYERS, D_MODEL, 2 * A_WIDTH), D_MODEL ** -0.5)
    a_b_in = nrm(ks[2], (N_A_LAYERS, 2 * A_WIDTH), 0.02)
    a_ln_v_g = 1.0 + nrm(ks[3], (N_A_LAYERS, A_WIDTH), 0.02)
    a_ln_v_b = nrm(ks[4], (N_A_LAYERS, A_WIDTH), 0.02)
    a_w_s = nrm(ks[5], (N_A_LAYERS, A_GROUPS, CHUNK, CHUNK), CHUNK ** -0.5)
    a_b_s = 1.0 + nrm(ks[6], (N_A_LAYERS, A_GROUPS, CHUNK), 0.1)
    a_w_out = nrm(ks[7], (N_A_LAYERS, A_WIDTH, D_MODEL), DEEPNORM_BETA * A_WIDTH ** -0.5)
    w_k = nrm(ks[8], (D_MODEL, N_KV_HEADS * HEAD_DIM), D_MODEL ** -0.5)
    w_v = nrm(ks[9], (D_MODEL, N_KV_HEADS * HEAD_DIM), DEEPNORM_BETA * D_MODEL ** -0.5)
    kv_w = jnp.concatenate([w_k, w_v], axis=-1)
    kv_b = nrm(ks[10], (2 * N_KV_HEADS * HEAD_DIM,), 0.02)
    b_w_q = nrm(ks[11], (N_B_LAYERS, D_MODEL, N_Q_HEADS * HEAD_DIM), D_MODEL ** -0.5)
    b_b_q = nrm(ks[12], (N_B_LAYERS, N_Q_HEADS * HEAD_DIM), 0.02)
    b_sinks = nrm(ks[13], (N_B_LAYERS, N_Q_HEADS), 1.0)
    b_w_o = nrm(ks[14], (N_B_LAYERS, N_Q_HEADS * HEAD_DIM, D_MODEL),
                DEEPNORM_BETA * (N_Q_HEADS * HEAD_DIM) ** -0.5)
    mlp_w_up = nrm(ks[15], (DEPTH, D_MODEL, D_FF), DEEPNORM_BETA * D_MODEL ** -0.5)
    mlp_w_down = nrm(ks[16], (DEPTH, D_FF, D_MODEL), DEEPNORM_BETA * D_FF ** -0.5)
    ln_g = 1.0 + nrm(ks[17], (DEPTH, 2, D_MODEL), 0.02)
    ln_b = nrm(ks[18], (DEPTH, 2, D_MODEL), 0.02)
    return {"x": x, "a_w_in": a_w_in, "a_b_in": a_b_in, "a_ln_v_g": a_ln_v_g,
            "a_ln_v_b": a_ln_v_b, "a_w_s": a_w_s, "a_b_s": a_b_s, "a_w_out": a_w_out,
            "kv_w": kv_w, "kv_b": kv_b, "b_w_q": b_w_q, "b_b_q": b_b_q,
            "b_sinks": b_sinks, "b_w_o": b_w_o, "mlp_w_up": mlp_w_up,
            "mlp_w_down": mlp_w_down, "ln_g": ln_g, "ln_b": ln_b}


def reference(x, a_w_in, a_b_in, a_ln_v_g, a_ln_v_b, a_w_s, a_b_s, a_w_out,
              kv_w, kv_b, b_w_q, b_b_q, b_sinks, b_w_o, mlp_w_up, mlp_w_down,
              ln_g, ln_b):
    S = x.shape[1]
    positions = jnp.arange(S, dtype=jnp.int32)
    k_band = v_band = None
    for layer in range(DEPTH):
        if layer < N_A_LAYERS:
            mix = chunked_sgu(x, a_w_in[layer], a_b_in[layer], a_ln_v_g[layer],
                              a_ln_v_b[layer], a_w_s[layer], a_b_s[layer], a_w_out[layer])
        else:
            if layer == N_A_LAYERS:
                k_band, v_band = shared_kv_bands(x, kv_w, kv_b, positions)
            j = layer - N_A_LAYERS
            mix = sliding_sink_attention(x, k_band, v_band, b_w_q[j], b_b_q[j],
                                         b_sinks[j], b_w_o[j], positions)
        x = layer_norm(DEEPNORM_ALPHA * x + mix, ln_g[layer, 0], ln_b[layer, 0])
        x = layer_norm(DEEPNORM_ALPHA * x + sq_relu_mlp(x, mlp_w_up[layer], mlp_w_down[layer]),
                       ln_g[layer, 1], ln_b[layer, 1])
    return x
```

```python
import math
from contextlib import ExitStack

import numpy as np
import concourse.bass as bass
import concourse.mybir as mybir
from concourse.bass_utils import run_bass_kernel_spmd

F32 = mybir.dt.float32
BF16 = mybir.dt.bfloat16
ACT = mybir.ActivationFunctionType
ALU = mybir.AluOpType
AX = mybir.AxisListType
P = 128
NCORES = 8


class Cfg:
    def __init__(self, D=2048, FF=8192, G=8, NKV=4, HD=64, S=16384, depth=4, tile_chunks=4,
                 ln_eps=1e-5, theta=10000.0, ncores=NCORES):
        self.ncores = ncores
        self.D = D
        self.A = D
        self.FF = FF
        self.G = G
        self.NKV = NKV
        self.HD = HD
        self.NQ = D // HD
        self.QPK = self.NQ // NKV
        self.S = S
        self.depth = depth
        self.NA = depth // 2
        self.NB = depth - self.NA
        self.alpha = (2.0 * depth) ** 0.25
        self.eps = ln_eps
        self.theta = theta
        self.DC = D // P
        self.KVC = 2 * NKV * HD
        self.own = S // ncores // P
        self.tile_chunks = tile_chunks
        assert self.own % tile_chunks == 0
        self.ntiles = self.own // tile_chunks
        self.NCHMAX = tile_chunks
        self.TMAX = self.NCHMAX * P
        self.FP = min(FF, 2048)
        self.nparts = FF // self.FP
        self.PW = min(512, D)
        self.KH = min(8, self.DC)


class Buf:
    __slots__ = ("name", "w", "r", "dsem", "dcnt")

    def __init__(self, name):
        self.name = name
        self.w = None
        self.r = []
        self.dsem = None
        self.dcnt = 0


SEM_ROT = 12000


class Stream:
    def __init__(self, prog, name, is_pe=False):
        self.prog = prog
        self.name = name
        self.ops = []
        self.sem = prog.new_sem()
        self.cnt = 0
        self.seen = {}
        self.is_pe = is_pe
        self.pending = []


class Prog:
    def __init__(self):
        self.nsem = 0
        self.streams = {}
        for n in ("pe", "act", "dve", "pool", "sp"):
            self.streams[n] = Stream(self, n, is_pe=(n == "pe"))

    def new_sem(self):
        self.nsem += 1
        return self.nsem - 1

    def _waits(self, st, reads, writes, extra):
        toks = []
        for b in reads:
            if b.w is not None:
                toks.append(b.w)
        for b in writes:
            if b.w is not None:
                toks.append(b.w)
            toks.extend(b.r)
        toks.extend(extra)
        best = {}
        for (s, v) in toks:
            if st.is_pe and s == st.sem:
                continue
            if v > best.get(s, 0):
                best[s] = v
        waits = []
        for s, v in best.items():
            if v > st.seen.get(s, 0):
                st.seen[s] = v
                waits.append((s, v))
        return waits

    def op(self, eng, meth, kw, reads=(), writes=(), extra=(), signal=True):
        fn = (meth, kw)
        st = self.streams[eng]
        waits = self._waits(st, reads, writes, extra)
        if st.cnt >= SEM_ROT:
            st.sem = self.new_sem()
            st.cnt = 0
        tok = (st.sem, st.cnt + 1)
        if signal:
            st.cnt += 1
            st.ops.append((fn, waits, st.sem))
        else:
            st.ops.append((fn, waits, None))
        for b in reads:
            b.r.append(tok)
        for b in writes:
            b.w = tok
            b.r = []
        return tok

    def flush_pe(self):
        pass

    def dma(self, eng, meth, kw, dst=None, src=None, reads=(), writes=(), extra=()):
        fn = (meth, kw)
        st = self.streams[eng]
        owner = dst if dst is not None else src
        if owner.dsem is None:
            owner.dsem = self.new_sem()
        rd = list(reads) + ([src] if src is not None else [])
        wr = list(writes) + ([dst] if dst is not None else [])
        waits = self._waits(st, rd, wr, extra)
        owner.dcnt += 16
        tok = (owner.dsem, owner.dcnt)
        st.ops.append((fn, waits, ("dma", owner.dsem)))
        for b in rd:
            b.r.append(tok)
        for b in wr:
            b.w = tok
            b.r = []
        return tok

    def final_wait(self, eng, toks):
        st = self.streams[eng]
        waits = self._waits(st, (), (), toks)
        st.ops.append((None, waits, None))

    def replay(self, nc, stack):
        sems = [stack.enter_context(nc.semaphore(f"s{i}")) for i in range(self.nsem)]
        block = stack.enter_context(nc.Block())

        def run(st):
            def body(eng):
                for fn, waits, sig in st.ops:
                    for (s, v) in waits:
                        eng.wait_ge(sems[s], v)
                    if fn is None:
                        continue
                    ins = getattr(eng, fn[0])(**fn[1])
                    if sig is None:
                        continue
                    if isinstance(sig, tuple):
                        ins.then_inc(sems[sig[1]], 16)
                    else:
                        ins.then_inc(sems[sig], 1)
            return body

        block.tensor(run(self.streams["pe"]))
        block.scalar(run(self.streams["act"]))
        block.vector(run(self.streams["dve"]))
        block.gpsimd(run(self.streams["pool"]))
        block.sync(run(self.streams["sp"]))


WNAMES = ["a_w_in", "a_w_out", "kv_w", "b_w_q", "b_w_o", "mlp_w_up", "mlp_w_down"]


def weight_shapes(cfg):
    return {
        "a_w_in": (cfg.NA * cfg.D, 2 * cfg.A),
        "a_w_out": (cfg.NA * cfg.A, cfg.D),
        "kv_w": (cfg.D, cfg.KVC),
        "b_w_q": (cfg.NB * cfg.D, cfg.D),
        "b_w_o": (cfg.NB * cfg.D, cfg.D),
        "mlp_w_up": (cfg.depth * cfg.D, cfg.FF),
        "mlp_w_down": (cfg.depth * cfg.FF, cfg.D),
    }


def build_program(cfg, gather=True):
    nc = bass.Bass("TRN2", target_bir_lowering=False)
    pr = Prog()
    D, DC, A, FF, G = cfg.D, cfg.DC, cfg.A, cfg.FF, cfg.G
    HD, NQ, NKV, QPK, KVC = cfg.HD, cfg.NQ, cfg.NKV, cfg.QPK, cfg.KVC
    NCHMAX, TMAX = cfg.NCHMAX, cfg.TMAX
    NALL = cfg.own + 1
    HH = HD // 2
    GD = A // G
    assert GD % P == 0 or GD >= P
    PW, KH = cfg.PW, cfg.KH
    alpha = float(cfg.alpha)
    scale = float(HD ** -0.5)

    def dram_in(name, shape):
        return nc.dram_tensor(name, list(shape), F32, kind="ExternalInput").ap()

    xin = dram_in("xin", [NALL, P, D])
    yout = nc.dram_tensor("yout", [cfg.own, P, D], F32, kind="ExternalOutput").ap()
    wsh = weight_shapes(cfg)
    wfull = {}
    wshard = {}
    for n in WNAMES:
        R, C = wsh[n]
        if gather:
            assert R % cfg.ncores == 0
            wshard[n] = dram_in(n, [R // cfg.ncores, C])
            wfull[n] = nc.dram_tensor(n + "_full", [R, C], F32, kind="Internal").ap()
        else:
            wfull[n] = dram_in(n, [R, C])
    p_bin_u = dram_in("p_bin_u", [cfg.NA, P, A // P])
    p_bin_v = dram_in("p_bin_v", [cfg.NA, 1, A])
    p_lnv_g = dram_in("p_lnv_g", [cfg.NA, 1, A])
    p_lnv_b = dram_in("p_lnv_b", [cfg.NA, 1, A])
    p_ws = dram_in("p_ws", [cfg.NA, P, G, P])
    p_bs = dram_in("p_bs", [cfg.NA, 1, G * P])
    p_kvb = dram_in("p_kvb", [1, KVC])
    p_bq = dram_in("p_bq", [cfg.NB, 1, D])
    p_sink = dram_in("p_sink", [cfg.NB, 1, NQ])
    p_lng = dram_in("p_lng", [cfg.depth * 2, 1, D])
    p_lnb = dram_in("p_lnb", [cfg.depth * 2, 1, D])
    c_ident = dram_in("c_ident", [P, P])
    c_tril = dram_in("c_tril", [P, P])
    c_mask = dram_in("c_mask", [P, 2, 2 * P])
    c_cos = dram_in("c_cos", [P, NALL, HH])
    c_sin = dram_in("c_sin", [P, NALL, HH])

    stack = ExitStack()

    def sb(name, shape, dt):
        return stack.enter_context(nc.sbuf_tensor(name, list(shape), dt))

    def ps(name, shape, dt):
        return stack.enter_context(nc.psum_tensor(name, list(shape), dt))

    X = sb("X", [P, NCHMAX, D], F32)
    XT = sb("XT", [P, DC, TMAX], BF16)
    BIG = sb("BIG", [P, NCHMAX, D], BF16)
    HT = sb("HT", [P, cfg.FP // P, TMAX], BF16)
    NSLOT = 4
    WR = [sb(f"WR{i}", [P, KH, PW], BF16) for i in range(NSLOT)]
    NKS = NCHMAX + 2
    NVAR = NKV
    KT = sb("KT", [P, NKS, NVAR, P], BF16)
    VS = sb("VS", [P, NKS, NKV * HD], BF16)
    NTMP = 3
    TMPB = [sb(f"TMPB{i}", [P, max(D, 2 * NKV * HD)], BF16) for i in range(NTMP)]
    QT = [sb(f"QT{i}", [P, DC, P], BF16) for i in range(2)]
    LNG = sb("LNG", [P, D], F32)
    LNB = sb("LNB", [P, D], F32)
    BIAS = sb("BIAS", [P, max(D, KVC)], F32)
    VTMP = BIAS
    NF = 3
    FT = [sb(f"FT{i}", [P, 512], F32) for i in range(NF)]
    RT = FT[0:2]
    ident_f = sb("ident_f", [P, P], F32)
    ident = sb("ident", [P, P], BF16)
    tril = sb("tril", [P, P], F32)
    maskf = sb("maskf", [P, 2, 2 * P], F32)
    maskb = sb("maskb", [P, 2, 2 * P], BF16)
    cosT = sb("cosT", [P, NCHMAX, HH], F32)
    sinT = sb("sinT", [P, NCHMAX, HH], F32)
    ones1 = sb("ones1", [1, P], BF16)
    binu = sb("binu", [P, A // P], F32)
    wsb = sb("wsb", [P, G, P], BF16)
    WST = sb("WST", [P, G, P], BF16)
    bsb = sb("bsb", [1, G * P], BF16)
    sinkb = sb("sinkb", [P, NQ], F32)
    stats = [sb(f"stats{i}", [P, 4, 6], F32) for i in range(NCHMAX + 1)]
    mv = [sb(f"mv{i}", [P, 2], F32) for i in range(2)]
    sc1 = [sb(f"sc1_{i}", [P, 2], F32) for i in range(2)]
    PEX = [sb(f"PEX{i}", [P, 2, 2 * P], BF16) for i in range(3)]
    PTS = [sb(f"PTS{i}", [P, 2, 2, P], BF16) for i in range(3)]
    smx = [sb(f"smx{i}", [P, 8], F32) for i in range(3)]
    RINV = sb("RINV", [P, NQ], F32)
    NRH = max(8, NKV)
    rtq = [sb(f"rtq{i}", [P, NRH, HH], F32) for i in range(4)]

    NPR = 6
    PSR = [ps(f"PSR{i}", [P, 512], F32) for i in range(NPR)]
    PSO = [ps(f"PSO{i}", [P, 512], F32) for i in range(2)]

    bX = [Buf(f"X{i}") for i in range(NCHMAX)]
    bXT = [Buf(f"XT{i}") for i in range(NCHMAX)]
    bBIG = [Buf(f"BIG{i}") for i in range(NCHMAX)]
    bHT = Buf("HT")
    bWR = [Buf(f"WR{i}") for i in range(NSLOT)]
    bKV = [Buf(f"KV{i}") for i in range(NKS)]
    bTMPB = [Buf(f"TMPB{i}") for i in range(NTMP)]
    bQT = [Buf(f"QT{i}") for i in range(2)]
    bLN = Buf("LN")
    bBIAS = Buf("BIAS")
    bVTMP = bBIAS
    bFT = [Buf(f"FT{i}") for i in range(NF)]
    bRT = bFT[0:2]
    bPSR = [Buf(f"PSR{i}") for i in range(NPR)]
    bPSO = [Buf(f"PSO{i}") for i in range(2)]
    bconst = Buf("const")
    brope = Buf("rope")
    bstats = [Buf(f"stats{i}") for i in range(NCHMAX + 1)]
    bmv = [Buf(f"mv{i}") for i in range(2)]
    bsc1 = [Buf(f"sc1{i}") for i in range(2)]
    bPEX = [Buf(f"PEX{i}") for i in range(3)]
    bPTS = [Buf(f"PTS{i}") for i in range(3)]
    bsmx = [Buf(f"smx{i}") for i in range(3)]
    bRINV = Buf("RINV")
    brt = [Buf(f"rt{i}") for i in range(4)]
    bsgu = Buf("sguparams")
    bsink = Buf("sink")
    bwfull = {n: Buf("wf_" + n) for n in WNAMES}

    rr = {"psr": 0, "wr": 0, "tmpb": 0, "ft": 0, "rt": 0, "qt": 0, "mv": 0, "pex": 0, "pso": 0}

    def nxt(key, n):
        i = rr[key]
        rr[key] = (i + 1) % n
        return i

    def V(meth, reads=(), writes=(), **kw):
        return pr.op("dve", meth, kw, reads, writes)

    def S(meth, reads=(), writes=(), **kw):
        return pr.op("act", meth, kw, reads, writes)

    def T(meth, reads=(), writes=(), signal=True, **kw):
        return pr.op("pe", meth, kw, reads, writes, signal=signal)

    def DMA(eng, out, in_, dst=None, src=None, reads=()):
        return pr.dma(eng, "dma_start", dict(out=out, in_=in_), dst=dst, src=src, reads=reads)

    DMA("sp", ident_f[:, :], c_ident, dst=bconst)
    DMA("sp", tril[:, :], c_tril, dst=bconst)
    DMA("sp", maskf[:, :, :], c_mask, dst=bconst)
    V("tensor_copy", reads=[bconst], writes=[bconst], out=ident[:, :], in_=ident_f[:, :])
    V("tensor_copy", reads=[bconst], writes=[bconst], out=maskb[:, :, :], in_=maskf[:, :, :])
    V("memset", writes=[bconst], ap=ones1[:, :], constant=1.0)

    if gather:
        order = ["a_w_in", "a_w_out", "mlp_w_up", "mlp_w_down", "kv_w", "b_w_q", "b_w_o"]
        for n in order:
            R, C = wsh[n]
            bounce = nc.dram_tensor(n + "_bnc", [R // cfg.ncores, C], F32, kind="Internal").ap()
            bb = Buf("bnc_" + n)
            DMA("pool", bounce, wshard[n], dst=bb)
            pr.dma("pool", "collective_compute",
                   dict(kind="AllGather", op=ALU.bypass, replica_groups=[list(range(cfg.ncores))],
                        ins=[bounce], outs=[wfull[n]]),
                   dst=bwfull[n], reads=[bb])

    def tok_ranges(ncols):
        nsp = (ncols + 511) // 512
        base = ncols // nsp
        assert base * nsp == ncols
        return [(i * base, base) for i in range(nsp)]

    class Piece:
        __slots__ = ("slot", "buf", "kc", "pw")

    def load_piece(wname, r0, kc, c0, pw):
        s = nxt("wr", NSLOT)
        src_ap = wfull[wname][r0:r0 + kc * P, c0:c0 + pw].rearrange("(k p) n -> p k n", p=P)
        DMA("pool", WR[s][:, 0:kc, 0:pw], src_ap, dst=bWR[s], reads=[bwfull[wname]])
        pc = Piece()
        pc.slot, pc.buf, pc.kc, pc.pw = s, bWR[s], kc, pw
        return pc

    def load_pieces(wname, r0, K, c0, pw):
        out = []
        kcs = K // P
        k = 0
        while k < kcs:
            kc = min(KH, kcs - k)
            out.append((k, load_piece(wname, r0 + k * P, kc, c0, pw)))
            k += kc
        return out

    def mm(out_ap, lhsT, rhs, start, stop, reads, writes, signal):
        return T("matmul", reads=reads, writes=writes, signal=signal,
                 out=out_ap, lhsT=lhsT, rhs=rhs, start=start, stop=stop)

    def bcast_load(dst_tile, width, src_row_ap, dstbuf):
        return DMA("sp", dst_tile[:, 0:width], src_row_ap.to_broadcast([P, width]), dst=dstbuf)

    def proj_ws(wname, r0, K, c0, ncols, srcT, src_bufs, col_lo, ncol_tok, evac):
        kcs = K // P
        for cp in range(0, ncols, PW):
            pw = min(PW, ncols - cp)
            pieces = load_pieces(wname, r0, K, c0 + cp, pw)
            for mm_i in range(pw // P):
                m = (cp // P) + mm_i
                outs = []
                trs = tok_ranges(ncol_tok)
                banks = [nxt("psr", NPR) for _ in trs]
                for ti, (t0, tn) in enumerate(trs):
                    bk = banks[ti]
                    for (k0, pc) in pieces:
                        for kk in range(pc.kc):
                            k = k0 + kk
                            last = (k == kcs - 1)
                            mm(PSR[bk][:, 0:tn], WR[pc.slot][:, kk, mm_i * P:(mm_i + 1) * P],
                               srcT[:, k, col_lo + t0:col_lo + t0 + tn],
                               start=(k == 0), stop=last,
                               reads=[pc.buf] + (src_bufs if k == 0 else []), writes=[bPSR[bk]],
                               signal=last)
                    outs.append((PSR[bk][:, 0:tn], bPSR[bk], t0, tn))
                evac(m, outs)

    def proj_as(wname, r0, K, c0, ncols, srcT, src_bufs, slots, evac):
        kcs = K // P
        for cp in range(0, ncols, PW):
            pw = min(PW, ncols - cp)
            pieces = load_pieces(wname, r0, K, c0 + cp, pw)
            for c in slots:
                bk = nxt("psr", NPR)
                for (k0, pc) in pieces:
                    for kk in range(pc.kc):
                        k = k0 + kk
                        last = (k == kcs - 1)
                        mm(PSR[bk][:, 0:pw], srcT[:, k, c * P:(c + 1) * P], WR[pc.slot][:, kk, 0:pw],
                           start=(k == 0), stop=last,
                           reads=[pc.buf] + ([src_bufs[c]] if k == 0 else []), writes=[bPSR[bk]],
                           signal=(last or kk == pc.kc - 1))
                evac(c, cp, pw, PSR[bk][:, 0:pw], bPSR[bk])

    def transpose_to(src_tile, src_buf, dstT, dst_buf, col0, nfc=None):
        nfc = nfc if nfc is not None else DC
        for f0 in range(0, nfc, 4):
            nf = min(4, nfc - f0)
            bk = nxt("psr", NPR)
            pt = PSR[bk][:, :].bitcast(BF16)
            for j in range(nf):
                T("transpose", reads=[src_buf, bconst] if j == 0 else [], writes=[bPSR[bk]], signal=(j == nf - 1),
                  out=pt[:, j * P:(j + 1) * P], in_=src_tile[:, (f0 + j) * P:(f0 + j + 1) * P], identity=ident[:, :])
            S("copy", reads=[bPSR[bk]], writes=[dst_buf],
              out=dstT[:, f0:f0 + nf, col0:col0 + P],
              in_=pt[:, 0:nf * P].rearrange("p (f t) -> p f t", t=P))

    def load_ln(idx):
        bcast_load(LNG, D, p_lng[idx], bLN)
        bcast_load(LNB, D, p_lnb[idx], bLN)

    def ln_scalars(src_ap, src_buf, width, pre_stats=None):
        i = nxt("mv", 2)
        if pre_stats is None:
            st_t, st_b = stats[NCHMAX], bstats[NCHMAX]
            nst = (width + 511) // 512
            for q in range(nst):
                w0 = q * 512
                wn = min(512, width - w0)
                V("bn_stats", reads=[src_buf], writes=[st_b], out=st_t[:, q, :], in_=src_ap[:, w0:w0 + wn])
        else:
            st_t, st_b, nst = pre_stats
        V("bn_aggr", reads=[st_b], writes=[bmv[i]], out=mv[i][:, :], in_=st_t[:, 0:nst, :])
        V("tensor_scalar", reads=[bmv[i]], writes=[bsc1[i]], out=sc1[i][:, 0:1], in0=mv[i][:, 1:2],
          scalar1=float(cfg.eps), scalar2=None, op0=ALU.add)
        S("activation", reads=[bsc1[i]], writes=[bsc1[i]], out=sc1[i][:, 0:1], in_=sc1[i][:, 0:1], func=ACT.Sqrt)
        V("reciprocal", reads=[bsc1[i]], writes=[bsc1[i]], out=sc1[i][:, 0:1], in_=sc1[i][:, 0:1])
        V("scalar_tensor_tensor", reads=[bmv[i], bsc1[i]], writes=[bsc1[i]], out=sc1[i][:, 1:2], in0=mv[i][:, 0:1],
          scalar=-1.0, in1=sc1[i][:, 0:1], op0=ALU.mult, op1=ALU.mult)
        return i

    def ln_residual(c):
        i = ln_scalars(X[:, c, :], bX[c], D)
        xc = X[:, c, :]
        S("activation", reads=[bsc1[i]], writes=[bX[c]], out=xc, in_=xc, func=ACT.Identity,
          bias=sc1[i][:, 1:2], scale=sc1[i][:, 0:1])
        V("tensor_tensor", reads=[bLN], writes=[bX[c]], out=xc, in0=xc, in1=LNG[:, 0:D], op=ALU.mult)
        V("tensor_tensor", reads=[bLN], writes=[bX[c]], out=xc, in0=xc, in1=LNB[:, 0:D], op=ALU.add)

    def x_to_xt(c):
        t = nxt("tmpb", NTMP)
        S("copy", reads=[bX[c]], writes=[bTMPB[t]], out=TMPB[t][:, 0:D], in_=X[:, c, :])
        transpose_to(TMPB[t], bTMPB[t], XT, bXT[c], c * P)

    def ln_and_transpose(c):
        ln_residual(c)
        x_to_xt(c)

    def resid_evac(first):
        def ev(c, n0, pw, psum_ap, pbuf):
            xs = X[:, c, n0:n0 + pw]
            if first:
                V("scalar_tensor_tensor", reads=[pbuf], writes=[bX[c]], out=xs, in0=xs, scalar=alpha, in1=psum_ap,
                  op0=ALU.mult, op1=ALU.add)
            else:
                V("tensor_tensor", reads=[pbuf], writes=[bX[c]], out=xs, in0=xs, in1=psum_ap, op=ALU.add)
        return ev

    def mlp(layer, slots):
        lo = slots[0]
        ntok = len(slots) * P
        col_lo = lo * P
        for j in range(cfg.nparts):
            def ev_up(m, outs):
                for (psum_ap, pbuf, t0, tn) in outs:
                    r = nxt("rt", 2)
                    S("activation", reads=[pbuf], writes=[bRT[r]], out=RT[r][:, 0:tn], in_=psum_ap, func=ACT.Relu)
                    V("tensor_tensor", reads=[bRT[r]], writes=[bHT], out=HT[:, m, col_lo + t0:col_lo + t0 + tn],
                      in0=RT[r][:, 0:tn], in1=RT[r][:, 0:tn], op=ALU.mult)
            proj_ws("mlp_w_up", layer * D, D, j * cfg.FP, cfg.FP, XT, [bXT[c] for c in slots], col_lo, ntok, ev_up)
            proj_as("mlp_w_down", layer * FF + j * cfg.FP, cfg.FP, 0, D, HT, {c: bHT for c in slots}, slots,
                    resid_evac(j == 0))

    def sgu_params(l):
        DMA("pool", binu[:, :], p_bin_u[l], dst=bsgu)
        DMA("pool", wsb[:, :, :], p_ws[l], dst=bsgu)
        DMA("pool", bsb[:, :], p_bs[l], dst=bsgu)
        V("tensor_tensor", reads=[bsgu, bconst], writes=[bsgu], out=wsb[:, :, :], in0=wsb[:, :, :],
          in1=tril[:, :].unsqueeze(1).to_broadcast([P, G, P]), op=ALU.mult)
        for g0 in range(0, G, 4):
            ng = min(4, G - g0)
            bk = nxt("psr", NPR)
            pt = PSR[bk][:, :].bitcast(BF16)
            for j in range(ng):
                T("transpose", reads=[bsgu, bconst] if j == 0 else [], writes=[bPSR[bk]], signal=(j == ng - 1),
                  out=pt[:, j * P:(j + 1) * P], in_=wsb[:, g0 + j, :], identity=ident[:, :])
            S("copy", reads=[bPSR[bk]], writes=[bsgu], out=WST[:, g0:g0 + ng, :],
              in_=pt[:, 0:ng * P].rearrange("p (f t) -> p f t", t=P))

    def sgu(l, slots):
        lo = slots[0]
        ntok = len(slots) * P
        col_lo = lo * P
        sgu_params(l)
        bcast_load(BIAS, A, p_bin_v[l], bBIAS)

        def ev_u(m, outs):
            for (psum_ap, pbuf, t0, tn) in outs:
                S("activation", reads=[pbuf, bsgu], writes=[bHT], out=HT[:, m, col_lo + t0:col_lo + t0 + tn],
                  in_=psum_ap, func=ACT.Gelu, bias=binu[:, m:m + 1])
        proj_ws("a_w_in", l * D, D, 0, A, XT, [bXT[c] for c in slots], col_lo, ntok, ev_u)

        def ev_v(c, n0, pw, psum_ap, pbuf):
            f = nxt("ft", NF)
            V("tensor_tensor", reads=[pbuf, bBIAS], writes=[bFT[f]], out=FT[f][:, 0:pw], in0=psum_ap,
              in1=BIAS[:, n0:n0 + pw], op=ALU.add)
            S("activation", writes=[bFT[f]], out=FT[f][:, 0:pw], in_=FT[f][:, 0:pw], func=ACT.Gelu)
            V("bn_stats", reads=[bFT[f]], writes=[bstats[c]], out=stats[c][:, n0 // PW, :], in_=FT[f][:, 0:pw])
            S("copy", reads=[bFT[f]], writes=[bBIG[c]], out=BIG[:, c, n0:n0 + pw], in_=FT[f][:, 0:pw])
        proj_as("a_w_in", l * D, D, A, A, XT, {c: bXT[c] for c in slots}, slots, ev_v)

        bcast_load(LNG, A, p_lnv_g[l], bLN)
        bcast_load(LNB, A, p_lnv_b[l], bLN)
        nst = (A + PW - 1) // PW

        def prep_vn(c):
            i = ln_scalars(None, None, A, pre_stats=(stats[c], bstats[c], nst))
            f = nxt("tmpb", NTMP)
            S("activation", reads=[bBIG[c], bsc1[i]], writes=[bVTMP], out=VTMP[:, 0:A], in_=BIG[:, c, 0:A],
              func=ACT.Identity, bias=sc1[i][:, 1:2], scale=sc1[i][:, 0:1])
            V("tensor_tensor", reads=[bLN], writes=[bVTMP], out=VTMP[:, 0:A], in0=VTMP[:, 0:A], in1=LNG[:, 0:A], op=ALU.mult)
            V("tensor_tensor", reads=[bLN, bVTMP], writes=[bTMPB[f]], out=TMPB[f][:, 0:A], in0=VTMP[:, 0:A],
              in1=LNB[:, 0:A], op=ALU.add)
            return f

        def mix(c, f):
            for m0 in range(0, A // P, 4):
                bk = nxt("psr", NPR)
                nm = min(4, A // P - m0)
                for mi in range(nm):
                    m = m0 + mi
                    g = (m * P) // GD
                    mm(PSR[bk][:, mi * P:(mi + 1) * P], TMPB[f][:, m * P:(m + 1) * P], WST[:, g, :],
                       start=True, stop=False, reads=[bTMPB[f], bsgu] if mi == 0 else [], writes=[bPSR[bk]], signal=False)
                    mm(PSR[bk][:, mi * P:(mi + 1) * P], ones1[0:1, :], bsb[0:1, g * P:(g + 1) * P],
                       start=False, stop=True, reads=[bconst] if mi == 0 else [], writes=[bPSR[bk]], signal=(mi == nm - 1))
                V("tensor_tensor", reads=[bPSR[bk], bHT], writes=[bXT[c]],
                  out=XT[:, m0:m0 + nm, c * P:(c + 1) * P],
                  in0=PSR[bk][:, 0:nm * P].rearrange("p (f t) -> p f t", t=P),
                  in1=HT[:, m0:m0 + nm, c * P:(c + 1) * P], op=ALU.mult)

        fprev = prep_vn(slots[0])
        for ci, c in enumerate(slots):
            fnext = prep_vn(slots[ci + 1]) if ci + 1 < len(slots) else None
            mix(c, fprev)
            fprev = fnext
        load_ln(2 * l)
        proj_as("a_w_out", l * A, A, 0, D, XT, {c: bXT[c] for c in slots}, slots, resid_evac(True))
        for c in slots:
            ln_and_transpose(c)

    def rope(src_ap, nh, gc, dst_pairs, reads, writes):
        sv = src_ap.rearrange("p (h t d) -> p h t d", h=nh, t=2)
        x1 = sv[:, :, 0, :]
        x2 = sv[:, :, 1, :]
        cb = cosT[:, gc, :].unsqueeze(1).to_broadcast([P, nh, HH])
        sbb = sinT[:, gc, :].unsqueeze(1).to_broadcast([P, nh, HH])
        t0, t1, t2, t3 = [t[:, 0:nh, :] for t in rtq]
        b0, b1, b2, b3 = brt
        V("tensor_tensor", reads=list(reads) + [brope], writes=[b0], out=t0, in0=x1, in1=cb, op=ALU.mult)
        V("tensor_tensor", reads=list(reads) + [brope], writes=[b1], out=t1, in0=x2, in1=sbb, op=ALU.mult)
        V("tensor_tensor", reads=list(reads) + [brope], writes=[b2], out=t2, in0=x2, in1=cb, op=ALU.mult)
        V("tensor_tensor", reads=list(reads) + [brope], writes=[b3], out=t3, in0=x1, in1=sbb, op=ALU.mult)
        for (o1, o2) in dst_pairs:
            V("tensor_tensor", reads=[b0, b1], writes=writes, out=o1, in0=t0, in1=t1, op=ALU.subtract)
            V("tensor_tensor", reads=[b2, b3], writes=writes, out=o2, in0=t2, in1=t3, op=ALU.add)

    def kv_slot(gc):
        return gc % NKS

    def kv_project(slots, gcs):
        bcast_load(BIAS, KVC, p_kvb[0:1, :], bBIAS)
        KW = NKV * HD

        def ev_kv(c, n0, pw, psum_ap, pbuf):
            gc = gcs[c]
            s = kv_slot(gc)
            f = nxt("ft", NF)
            V("tensor_tensor", reads=[pbuf, bBIAS], writes=[bFT[f]], out=FT[f][:, 0:pw], in0=psum_ap,
              in1=BIAS[:, n0:n0 + pw], op=ALU.add)
            S("copy", reads=[bFT[f]], writes=[bKV[s]], out=VS[:, s, :], in_=FT[f][:, KW:2 * KW])
            t = nxt("tmpb", NTMP)
            kview = TMPB[t][:, 0:2 * KW].rearrange("p (v h t d) -> p v h t d", v=2, h=NKV, t=2)
            st = kview[:, 0]
            rope(FT[f][:, 0:KW], NKV, c, [(st[:, :, 0, :], st[:, :, 1, :])], [bFT[f]], [bTMPB[t]])
            stp = TMPB[t][:, 0:KW].rearrange("p (a b e) -> p a b e", b=2, e=HD)
            swp = TMPB[t][:, KW:2 * KW].rearrange("p (a b e) -> p a b e", b=2, e=HD)
            for b in range(2):
                V("tensor_copy", reads=[bTMPB[t]], writes=[bTMPB[t]], out=swp[:, :, b, :], in_=stp[:, :, 1 - b, :])
            bk = nxt("psr", NPR)
            pt = PSR[bk][:, :].bitcast(BF16)
            for v in range(NVAR):
                T("transpose", reads=[bTMPB[t], bconst] if v == 0 else [], writes=[bPSR[bk]], signal=(v == NVAR - 1),
                  out=pt[:, v * P:(v + 1) * P], in_=TMPB[t][:, v * P:(v + 1) * P], identity=ident[:, :])
            S("copy", reads=[bPSR[bk]], writes=[bKV[s]], out=KT[:, s, :, :],
              in_=pt[:, 0:NVAR * P].rearrange("p (f t) -> p f t", t=P))
        assert KVC <= PW
        proj_as("kv_w", 0, D, 0, KVC, XT, {c: bXT[c] for c in slots}, slots, ev_kv)

    def attention(j, slots, gcs, first_core_chunk):
        bcast_load(BIAS, D, p_bq[j], bBIAS)
        DMA("sp", sinkb[:, :], p_sink[j].to_broadcast([P, NQ]), dst=bsink)

        def ev_q(c, n0, pw, psum_ap, pbuf):
            f = nxt("ft", NF)
            V("tensor_tensor", reads=[pbuf, bBIAS], writes=[bFT[f]], out=FT[f][:, 0:pw], in0=psum_ap,
              in1=BIAS[:, n0:n0 + pw], op=ALU.add)
            nh = pw // HD
            qv = BIG[:, c, n0:n0 + pw].rearrange("p (h t d) -> p h t d", h=nh, t=2)
            rope(FT[f][:, 0:pw], nh, c, [(qv[:, :, 0, :], qv[:, :, 1, :])], [bFT[f]], [bBIG[c]])
        proj_as("b_w_q", j * D, D, 0, D, XT, {c: bXT[c] for c in slots}, slots, ev_q)
        load_ln(2 * (cfg.NA + j))
        npair_h = NQ // 2
        for c in slots:
            gc = gcs[c]
            s_cur = kv_slot(gc)
            s_prev = kv_slot(gc - 1)
            mi = 0 if gc == first_core_chunk else 1
            qi = nxt("qt", 2)
            transpose_to(BIG[:, c, :], bBIG[c], QT[qi], bQT[qi], 0)
            t_o = nxt("tmpb", NTMP)

            def stageA(hp):
                bk = nxt("psr", NPR)
                for half in range(2):
                    h = 2 * hp + half
                    g = h // QPK
                    var = (g // 2) if (g % 2 == half) else (NKV // 2 + g // 2)
                    qop = QT[qi][half * HD:(half + 1) * HD, hp, :]
                    c0 = half * 2 * P
                    mm(PSR[bk][:, c0:c0 + 2 * P], ident[:, :], maskb[:, mi, :], start=True, stop=False,
                       reads=[bconst], writes=[bPSR[bk]], signal=False)
                    for bi, sl in enumerate((s_prev, s_cur)):
                        mm(PSR[bk][:, c0 + bi * P:c0 + (bi + 1) * P], qop,
                           KT[half * HD:(half + 1) * HD, sl, var, :], start=False, stop=(bi == 1),
                           reads=[bQT[qi], bKV[sl]], writes=[bPSR[bk]], signal=(half == 1 and bi == 1))
                return bk

            def stageB(hp, bk):
                x = nxt("pex", 3)
                sv = PSR[bk][:, 0:4 * P].rearrange("p (h k) -> p h k", h=2)
                m = smx[x]
                V("tensor_reduce", reads=[bPSR[bk]], writes=[bsmx[x]], out=m[:, 0:2], in_=sv, axis=AX.X, op=ALU.max)
                V("scalar_tensor_tensor", reads=[bsmx[x], bsink], writes=[bsmx[x]], out=m[:, 2:4], in0=m[:, 0:2],
                  scalar=scale, in1=sinkb[:, 2 * hp:2 * hp + 2], op0=ALU.mult, op1=ALU.max)
                V("tensor_scalar", reads=[bsmx[x]], writes=[bsmx[x]], out=m[:, 4:6], in0=m[:, 2:4],
                  scalar1=-1.0, scalar2=None, op0=ALU.mult)
                V("tensor_tensor", reads=[bsmx[x], bsink], writes=[bsmx[x]], out=m[:, 2:4],
                  in0=sinkb[:, 2 * hp:2 * hp + 2], in1=m[:, 2:4], op=ALU.subtract)
                V("memset", writes=[bsmx[x]], ap=m[:, 6:8], constant=0.0)
                for half in range(2):
                    S("activation", reads=[bPSR[bk], bsmx[x]], writes=[bPEX[x], bsmx[x]],
                      out=PEX[x][:, half, :], in_=PSR[bk][:, half * 2 * P:(half + 1) * 2 * P], func=ACT.Exp,
                      bias=m[:, 4 + half:5 + half], scale=scale, accum_out=m[:, 6 + half:7 + half])
                S("activation", reads=[bsmx[x]], writes=[bsmx[x]], out=m[:, 2:4], in_=m[:, 2:4], func=ACT.Exp)
                V("tensor_tensor", reads=[bsmx[x]], writes=[bsmx[x]], out=m[:, 6:8], in0=m[:, 6:8], in1=m[:, 2:4], op=ALU.add)
                V("reciprocal", reads=[bsmx[x]], writes=[bRINV], out=RINV[:, 2 * hp:2 * hp + 2], in_=m[:, 6:8])
                return x

            def stageC(hp, x, ob):
                bk = nxt("psr", NPR)
                pt = PSR[bk][:, :].bitcast(BF16)
                for half in range(2):
                    for bi in range(2):
                        idx = half * 2 + bi
                        T("transpose", reads=[bPEX[x], bconst] if idx == 0 else [], writes=[bPSR[bk]], signal=(idx == 3),
                          out=pt[:, idx * P:(idx + 1) * P], in_=PEX[x][:, half, bi * P:(bi + 1) * P], identity=ident[:, :])
                S("copy", reads=[bPSR[bk]], writes=[bPTS[x]], out=PTS[x][:, :, :, :],
                  in_=pt[:, 0:4 * P].rearrange("p (h b q) -> p h b q", h=2, b=2))
                for half in range(2):
                    h = 2 * hp + half
                    g = h // QPK
                    hl = h % 8
                    for bi, sl in enumerate((s_prev, s_cur)):
                        mm(PSO[ob][:, hl * HD:(hl + 1) * HD], PTS[x][:, half, bi, :], VS[:, sl, g * HD:(g + 1) * HD],
                           start=(bi == 0), stop=(bi == 1), reads=[bPTS[x], bKV[sl]], writes=[bPSO[ob]],
                           signal=(bi == 1 and half == 1))

            def flushO(h0, nh, ob):
                V("tensor_tensor", reads=[bPSO[ob], bRINV], writes=[bTMPB[t_o]],
                  out=TMPB[t_o][:, h0 * HD:(h0 + nh) * HD].rearrange("p (h d) -> p h d", d=HD),
                  in0=PSO[ob][:, 0:nh * HD].rearrange("p (h d) -> p h d", d=HD),
                  in1=RINV[:, h0:h0 + nh].unsqueeze(2).to_broadcast([P, nh, HD]), op=ALU.mult)

            pend = []
            ob = nxt("pso", 2)

            def drain_one(ob):
                php, px = pend.pop(0)
                stageC(php, px, ob)
                if (2 * php + 2) % 8 == 0 or php == npair_h - 1:
                    h0 = (2 * php // 8) * 8
                    flushO(h0, 2 * php + 2 - h0, ob)
                    return nxt("pso", 2)
                return ob

            for hp in range(npair_h):
                bk = stageA(hp)
                x = stageB(hp, bk)
                pend.append((hp, x))
                if len(pend) > 2:
                    ob = drain_one(ob)
            while pend:
                ob = drain_one(ob)
            transpose_to(TMPB[t_o], bTMPB[t_o], XT, bXT[c], c * P)
        proj_as("b_w_o", j * D, D, 0, D, XT, {c: bXT[c] for c in slots}, slots, resid_evac(True))
        for c in slots:
            ln_and_transpose(c)

    out_toks = []
    tiles = [("halo", [0], {0: 0})]
    for t in range(cfg.ntiles):
        sl = list(range(cfg.tile_chunks))
        tiles.append(("main", sl, {s_: 1 + t * cfg.tile_chunks + s_ for s_ in sl}))
    for kind, slotsA, gcs in tiles:
        n = len(slotsA)
        gc0 = gcs[slotsA[0]]
        DMA("sp", cosT[:, 0:n, :], c_cos[:, gc0:gc0 + n, :], dst=brope)
        DMA("sp", sinT[:, 0:n, :], c_sin[:, gc0:gc0 + n, :], dst=brope)
        for s_ in slotsA:
            DMA("sp", X[:, s_, :], xin[gcs[s_]], dst=bX[s_])
        for s_ in slotsA:
            x_to_xt(s_)
        for l in range(cfg.NA):
            sgu(l, slotsA)
            load_ln(2 * l + 1)
            mlp(l, slotsA)
            for c in slotsA:
                ln_and_transpose(c)
        kv_project(slotsA, gcs)
        if kind == "halo":
            continue
        slotsB = slotsA
        for j in range(cfg.NB):
            layer = cfg.NA + j
            attention(j, slotsB, gcs, first_core_chunk=1)
            load_ln(2 * layer + 1)
            mlp(layer, slotsB)
            last = (j == cfg.NB - 1)
            for c in slotsB:
                if last:
                    ln_residual(c)
                    out_toks.append(DMA("sp", yout[gcs[c] - 1], X[:, c, :], src=bX[c]))
                else:
                    ln_and_transpose(c)
    pr.final_wait("sp", out_toks)

    with nc.allow_low_precision("bf16 matmul operands, fp32 accumulation"):
        pr.replay(nc, stack)
    stack.close()
    return nc


def host_consts(cfg, core):
    HH = cfg.HD // 2
    NALL = cfg.own + 1
    ident = np.eye(P, dtype=np.float32)
    tril = np.tril(np.ones((P, P), np.float32))
    i = np.arange(P)[:, None]
    j = np.arange(2 * P)[None, :]
    band = (j > i) & (j <= i + P)
    NEG = -30000.0
    generic = np.where(band, 0.0, NEG).astype(np.float32)
    first = generic.copy()
    if core == 0:
        first[:, :P] = NEG
    mask = np.ascontiguousarray(np.stack([first, generic], axis=1))
    pos0 = core * cfg.own * P - P
    pos = (pos0 + np.arange(NALL * P)).astype(np.float32)
    inv = (np.float32(cfg.theta) ** (-np.arange(0, cfg.HD, 2, dtype=np.float32) / np.float32(cfg.HD))).astype(np.float32)
    ang = pos[:, None] * inv[None, :]
    cos = np.cos(ang).astype(np.float32).reshape(NALL, P, HH).transpose(1, 0, 2)
    sin = np.sin(ang).astype(np.float32).reshape(NALL, P, HH).transpose(1, 0, 2)
    return {"c_ident": ident, "c_tril": tril, "c_mask": mask,
            "c_cos": np.ascontiguousarray(cos), "c_sin": np.ascontiguousarray(sin)}


def make_in_maps(cfg, inputs, gather=True):
    D, A = cfg.D, cfg.A
    x = np.asarray(inputs["x"], np.float32).reshape(cfg.S, D)
    w2d = {n: np.ascontiguousarray(np.asarray(inputs[n], np.float32).reshape(weight_shapes(cfg)[n])) for n in WNAMES}
    a_b_in = np.asarray(inputs["a_b_in"], np.float32)
    common = {
        "p_bin_u": np.ascontiguousarray(a_b_in[:, :A].reshape(cfg.NA, A // P, P).transpose(0, 2, 1)),
        "p_bin_v": np.ascontiguousarray(a_b_in[:, A:].reshape(cfg.NA, 1, A)),
        "p_lnv_g": np.asarray(inputs["a_ln_v_g"], np.float32).reshape(cfg.NA, 1, A),
        "p_lnv_b": np.asarray(inputs["a_ln_v_b"], np.float32).reshape(cfg.NA, 1, A),
        "p_ws": np.ascontiguousarray(np.asarray(inputs["a_w_s"], np.float32).transpose(0, 2, 1, 3)),
        "p_bs": np.asarray(inputs["a_b_s"], np.float32).reshape(cfg.NA, 1, cfg.G * P),
        "p_kvb": np.asarray(inputs["kv_b"], np.float32).reshape(1, cfg.KVC),
        "p_bq": np.asarray(inputs["b_b_q"], np.float32).reshape(cfg.NB, 1, D),
        "p_sink": np.asarray(inputs["b_sinks"], np.float32).reshape(cfg.NB, 1, cfg.NQ),
        "p_lng": np.asarray(inputs["ln_g"], np.float32).reshape(cfg.depth * 2, 1, D),
        "p_lnb": np.asarray(inputs["ln_b"], np.float32).reshape(cfg.depth * 2, 1, D),
    }
    maps = []
    per = cfg.own * P
    for c in range(cfg.ncores):
        m = dict(common)
        xi = np.zeros((cfg.own + 1, P, D), np.float32)
        xi[1:] = x[c * per:(c + 1) * per].reshape(cfg.own, P, D)
        if c > 0:
            xi[0] = x[c * per - P:c * per]
        m["xin"] = xi
        for n in WNAMES:
            if gather:
                R = w2d[n].shape[0]
                m[n] = w2d[n][c * R // cfg.ncores:(c + 1) * R // cfg.ncores]
            else:
                m[n] = w2d[n]
        m.update(host_consts(cfg, c))
        maps.append(m)
    return maps


_CACHE = {}


def run(cfg, inputs, gather=True, trace=False):
    key = (cfg.D, cfg.FF, cfg.S, cfg.tile_chunks, gather, cfg.ncores)
    if key not in _CACHE:
        _CACHE[key] = build_program(cfg, gather)
    nc = _CACHE[key]
    maps = make_in_maps(cfg, inputs, gather)
    res = run_bass_kernel_spmd(nc, maps, core_ids=list(range(cfg.ncores)), trace=trace)
    out = np.concatenate([res.results[c]["yout"].reshape(cfg.own * P, cfg.D) for c in range(cfg.ncores)], axis=0)
    return out.reshape(1, cfg.S, cfg.D).astype(np.float32), res


NCORES_USED = 2


def kernel(**inputs):
    cfg = Cfg(ncores=NCORES_USED)
    out, _ = run(cfg, inputs, gather=False)
    return out
```
